# Optimizing a Trainium2 kernel written in Bass

```python
import math
import jax, jax.numpy as jnp
from jax import lax
import numpy as np

D_MODEL = 1024
BATCH = 2
SEQ = 8192
DEPTH = 1

HEAD_DIM = 64
N_HEADS = (D_MODEL // 2) // HEAD_DIM
N_KV_HEADS = max(1, N_HEADS // 4)
GROUP = N_HEADS // N_KV_HEADS
ATTN_WIDTH = N_HEADS * HEAD_DIM
KV_WIDTH = N_KV_HEADS * HEAD_DIM
WINDOW = 128
BLOCK = 128
POOL_WINDOWS = (2, 4, 8, 16)
N_POOL_GROUPS = len(POOL_WINDOWS)
POOL_WIDTH = D_MODEL // 2
POOL_GROUP_DIM = POOL_WIDTH // N_POOL_GROUPS
D_FF = 4 * D_MODEL
IN_WIDTH = POOL_WIDTH + ATTN_WIDTH + 2 * KV_WIDTH + 2 * D_MODEL
RMS_EPS = 1e-5
NEG_INF = -1e30
ALIBI_SLOPES = np.array([2.0 ** (-8.0 * (h + 1) / N_HEADS) for h in range(N_HEADS)], dtype=np.float32)

kernel_name = "hybrid_pool_swa_gated_block"


def rms_norm(x, g):
    xf = x.astype(jnp.float32)
    y = xf * lax.rsqrt(jnp.mean(xf * xf, axis=-1, keepdims=True) + RMS_EPS)
    return (y * g.astype(jnp.float32)).astype(x.dtype)


def pool_mixer(u, w_grp, b_grp, scale):
    B, S, _ = u.shape
    ug = u.reshape(B, S, N_POOL_GROUPS, POOL_GROUP_DIM).astype(jnp.float32)
    c = jnp.pad(jnp.cumsum(ug, axis=1), ((0, 0), (1, 0), (0, 0), (0, 0)))
    t = jnp.arange(S)[:, None]
    w = jnp.asarray(POOL_WINDOWS, dtype=jnp.int32)[None, :]
    lo = jnp.maximum(t + 1 - w, 0)
    grp = jnp.arange(N_POOL_GROUPS)[None, :]
    window_sum = c[:, 1:] - c[:, lo, grp]
    count = jnp.minimum(t + 1, w).astype(jnp.float32)
    d = (window_sum / count[None, :, :, None] - ug).astype(u.dtype)
    y = jnp.einsum('bsgc,gcd->bsgd', d, w_grp) + b_grp
    return y.reshape(B, S, POOL_WIDTH) * scale


def swa_sink_attention(q, k, v, sinks):
    B, S, _ = q.shape
    nb = S // BLOCK
    qb = q.reshape(B, nb, BLOCK, N_KV_HEADS, GROUP, HEAD_DIM)
    kb = k.reshape(B, nb, BLOCK, N_KV_HEADS, HEAD_DIM)
    vb = v.reshape(B, nb, BLOCK, N_KV_HEADS, HEAD_DIM)
    pad = ((0, 0), (1, 0), (0, 0), (0, 0), (0, 0))
    kk = jnp.concatenate([jnp.pad(kb, pad)[:, :-1], kb], axis=2)
    vv = jnp.concatenate([jnp.pad(vb, pad)[:, :-1], vb], axis=2)
    s = jnp.einsum('bnqhgd,bnkhd->bnhgqk', qb, kk, preferred_element_type=jnp.float32)
    s = s * (1.0 / math.sqrt(HEAD_DIM))
    qi = jnp.arange(BLOCK)[:, None]
    kj = jnp.arange(2 * BLOCK)[None, :]
    dist = BLOCK + qi - kj
    kpos = jnp.arange(nb)[:, None] * BLOCK - BLOCK + kj
    valid = ((dist >= 0) & (dist < WINDOW))[None] & (kpos >= 0)[:, None, :]
    slopes = jnp.asarray(ALIBI_SLOPES).reshape(N_KV_HEADS, GROUP)
    s = s - slopes[None, None, :, :, None, None] * dist.astype(jnp.float32)[None, None, None, None]
    s = jnp.where(valid[None, :, None, None], s, NEG_INF)
    sink = sinks.astype(jnp.float32).reshape(N_KV_HEADS, GROUP)[None, None, :, :, None, None]
    m = jnp.maximum(jnp.max(s, axis=-1, keepdims=True), sink)
    p = jnp.exp(s - m)
    p = p / (jnp.sum(p, axis=-1, keepdims=True) + jnp.exp(sink - m))
    o = jnp.einsum('bnhgqk,bnkhd->bnqhgd', p.astype(v.dtype), vv)
    return o.reshape(B, S, ATTN_WIDTH)


def setup_inputs(seed: int = 0) -> dict:
    key = jax.random.key(seed)
    ks = jax.random.split(key, 16)
    f32 = jnp.float32
    nrm = lambda k, shape, fan_in: jax.random.normal(k, shape, f32) * (fan_in ** -0.5)
    return {
        "x": jax.random.normal(ks[0], (BATCH, SEQ, D_MODEL), f32),
        "norm_mix": 1.0 + 0.1 * jax.random.normal(ks[1], (DEPTH, D_MODEL), f32),
        "w_in": nrm(ks[2], (DEPTH, D_MODEL, IN_WIDTH), D_MODEL),
        "pool_w": nrm(ks[3], (DEPTH, N_POOL_GROUPS, POOL_GROUP_DIM, POOL_GROUP_DIM), POOL_GROUP_DIM),
        "pool_b": 0.02 * jax.random.normal(ks[4], (DEPTH, N_POOL_GROUPS, POOL_GROUP_DIM), f32),
        "pool_scale": 1.0 + 0.1 * jax.random.normal(ks[5], (DEPTH, POOL_WIDTH), f32),
        "attn_sinks": 0.5 * jax.random.normal(ks[6], (DEPTH, N_HEADS), f32),
        "p_pool": nrm(ks[7], (DEPTH, POOL_WIDTH, D_MODEL), POOL_WIDTH),
        "p_attn": nrm(ks[8], (DEPTH, ATTN_WIDTH, D_MODEL), ATTN_WIDTH),
        "w_out": nrm(ks[9], (DEPTH, D_MODEL, D_MODEL), D_MODEL),
        "norm_mlp": 1.0 + 0.1 * jax.random.normal(ks[10], (DEPTH, D_MODEL), f32),
        "w_up": nrm(ks[11], (DEPTH, D_MODEL, D_FF), D_MODEL),
        "w_down": nrm(ks[12], (DEPTH, D_FF, D_MODEL), D_FF),
        "norm_final": 1.0 + 0.1 * jax.random.normal(ks[13], (D_MODEL,), f32),
    }


def reference(x, norm_mix, w_in, pool_w, pool_b, pool_scale, attn_sinks, p_pool, p_attn,
              w_out, norm_mlp, w_up, w_down, norm_final):
    h = x
    c1 = POOL_WIDTH
    c2 = c1 + ATTN_WIDTH
    c3 = c2 + KV_WIDTH
    c4 = c3 + KV_WIDTH
    c5 = c4 + D_MODEL
    for l in range(DEPTH):
        u = rms_norm(h, norm_mix[l])
        z = u @ w_in[l]
        u_pool = z[..., :c1]
        q = z[..., c1:c2]
        k = z[..., c2:c3]
        v = z[..., c3:c4]
        gate_pool = jax.nn.sigmoid(z[..., c4:c5])
        gate_attn = jax.nn.sigmoid(z[..., c5:])
        y_pool = pool_mixer(u_pool, pool_w[l], pool_b[l], pool_scale[l]) @ p_pool[l]
        y_attn = swa_sink_attention(q, k, v, attn_sinks[l]) @ p_attn[l]
        mixed = gate_pool * y_pool + gate_attn * y_attn
        h = h + mixed @ w_out[l]
        u2 = rms_norm(h, norm_mlp[l])
        a = jax.nn.relu(u2 @ w_up[l])
        h = h + (a * a) @ w_down[l]
    return rms_norm(h, norm_final)
```

```python
import numpy as np
from contextlib import ExitStack

import concourse.bass as bass
import concourse.mybir as mybir
from concourse.bass_utils import run_bass_kernel_spmd

F32 = mybir.dt.float32
BF16 = mybir.dt.bfloat16
ALU = mybir.AluOpType
AF = mybir.ActivationFunctionType
AX = mybir.AxisListType

NCORES = 8
D = 1024
T = 2048
NT = 16
DFF = 4096
EPS = 1e-5
SLOPES = [2.0 ** (-(h + 1)) for h in range(8)]
POOLW = (2, 4, 8, 16)

ARENA_F32 = 53000
BUCKET = 1024
POOL_ROWSUM = True
BIAS_TT = True
TS_MAX = False
FUSED_MAX = False


def _esize(dt):
    if dt == F32:
        return 4
    if dt == BF16:
        return 2
    raise ValueError(str(dt))


class Foot:
    __slots__ = ("space", "p0", "p1", "ivs")

    def __init__(self, space, p0, p1, ivs):
        self.space, self.p0, self.p1, self.ivs = space, p0, p1, ivs


def footprint(ap):
    t = ap.tensor
    tn = type(t).__name__
    if tn.startswith("DRam"):
        return None
    space = "ps" if tn.startswith("PSum") else "sb"
    es = _esize(ap.dtype)
    dims = [(int(s), int(c)) for (s, c) in ap.ap]
    off = int(ap.offset)
    pstep, pcnt = dims[0]
    assert pstep > 0
    p0 = off // pstep
    col = off % pstep
    free = dims[1:]
    if space == "ps":
        lo = col
        hi = col + sum((c - 1) * abs(s) for s, c in free) + 1
        b0 = (lo * es) // 2048
        b1 = (hi * es - 1) // 2048
        return Foot("ps", 0, 128, [(b0 * 2048, (b1 + 1) * 2048)])
    outer = 1
    for s, c in free[:-1]:
        outer *= c
    if len(free) == 0:
        ivs = [(col, col + 1)]
    elif outer > 512:
        hi = col + sum((c - 1) * abs(s) for s, c in free) + 1
        ivs = [(col, hi)]
    else:
        starts = [col]
        for s, c in free[:-1]:
            starts = [st + s * k for st in starts for k in range(c)]
        s, c = free[-1]
        ln = (c - 1) * abs(s) + 1
        ivs = [(st, st + ln) for st in starts]
    ivs = sorted((a * es, b * es) for a, b in ivs)
    merged = []
    for a, b in ivs:
        if merged and a <= merged[-1][1]:
            merged[-1] = (merged[-1][0], max(b, merged[-1][1]))
        else:
            merged.append((a, b))
    return Foot("sb", p0, p0 + pcnt, merged)


class Op:
    __slots__ = ("eng", "fn", "idx", "deps", "signal", "dma_slot", "dma_k", "sigval")

    def __init__(self, eng, fn):
        self.eng, self.fn = eng, fn
        self.deps = {}
        self.signal = False
        self.dma_slot = None
        self.dma_k = 0
        self.sigval = 0


class Prog:
    def __init__(self):
        self.ops = []
        self.per_eng = {"pe": [], "dve": [], "act": [], "pool": [], "sp": []}
        self.tr = {"sb": {}, "ps": {}}
        self.dma_count = {}

    def _conflicts(self, op, foot, is_write):
        tr = self.tr[foot.space]
        for (a, b) in foot.ivs:
            for bk in range(a // BUCKET, (b - 1) // BUCKET + 1):
                lst = tr.get(bk)
                if not lst:
                    continue
                lo = max(a, bk * BUCKET)
                hi = min(b, (bk + 1) * BUCKET)
                for e in lst:
                    if e[0] < hi and lo < e[1] and e[2] < foot.p1 and foot.p0 < e[3]:
                        if not (is_write or e[6]):
                            continue
                        key, n = e[4], e[5]
                        if op.eng == "pe" and key == "pe" and op.dma_slot is None:
                            continue
                        if op.deps.get(key, -1) < n:
                            op.deps[key] = n

    def _record(self, op, foot, is_write, key, n):
        tr = self.tr[foot.space]
        for (a, b) in foot.ivs:
            for bk in range(a // BUCKET, (b - 1) // BUCKET + 1):
                lo = max(a, bk * BUCKET)
                hi = min(b, (bk + 1) * BUCKET)
                lst = tr.setdefault(bk, [])
                if is_write:
                    lst[:] = [e for e in lst if not (lo <= e[0] and e[1] <= hi
                                                     and foot.p0 <= e[2] and e[3] <= foot.p1)]
                else:
                    lst[:] = [e for e in lst if not (e[4] == key and not e[6] and e[0] == lo
                                                     and e[1] == hi and e[2] == foot.p0
                                                     and e[3] == foot.p1)]
                lst.append([lo, hi, foot.p0, foot.p1, key, n, is_write])

    def op(self, eng, fn, r=(), w=(), dma_slot=None):
        op = Op(eng, fn)
        op.idx = len(self.ops)
        rf = [f for f in (footprint(a) for a in r if a is not None) if f is not None]
        wf = [f for f in (footprint(a) for a in w if a is not None) if f is not None]
        if dma_slot is not None:
            op.dma_slot = dma_slot
            k = self.dma_count.get(dma_slot, 0) + 1
            self.dma_count[dma_slot] = k
            op.dma_k = k
            key, n = ("dma", dma_slot), k
        else:
            key, n = eng, len(self.per_eng[eng])
        for f in rf:
            self._conflicts(op, f, f.space == "ps")
        for f in wf:
            self._conflicts(op, f, True)
        for f in rf:
            self._record(op, f, f.space == "ps", key, n)
        for f in wf:
            self._record(op, f, True, key, n)
        self.ops.append(op)
        self.per_eng[eng].append(op)
        return op

    def emit(self, nc, stack, final_waits):
        for op in self.ops:
            for key, n in op.deps.items():
                if isinstance(key, tuple):
                    continue
                self.per_eng[key][n].signal = True
        sems = {}
        for e in self.per_eng:
            sems[e] = stack.enter_context(nc.semaphore("s_" + e))
            c = 0
            for op in self.per_eng[e]:
                if op.dma_slot is None and op.signal:
                    c += 1
                    op.sigval = c
        dsem = {}
        for slot in self.dma_count:
            dsem[slot] = stack.enter_context(nc.semaphore("d_%s" % (slot,)))
        block = stack.enter_context(nc.Block())

        def stream(ename, eng):
            waited = {}
            for op in self.per_eng[ename]:
                for key, n in op.deps.items():
                    if isinstance(key, tuple):
                        sem, val = dsem[key[1]], 16 * n
                    else:
                        sem, val = sems[key], self.per_eng[key][n].sigval
                    if waited.get(key, -1) >= val:
                        continue
                    waited[key] = val
                    eng.wait_ge(sem, val)
                ins = op.fn(eng)
                if op.dma_slot is not None:
                    ins.then_inc(dsem[op.dma_slot], 16)
                elif op.signal:
                    ins.then_inc(sems[ename], 1)
            if ename == "sp":
                for slot in final_waits:
                    eng.wait_ge(dsem[slot], 16 * self.dma_count[slot])

        @block.tensor
        def _(e):
            stream("pe", e)

        @block.vector
        def _(e):
            stream("dve", e)

        @block.scalar
        def _(e):
            stream("act", e)

        @block.gpsimd
        def _(e):
            stream("pool", e)

        @block.sync
        def _(e):
            stream("sp", e)


class _Stop(Exception):
    pass


def build_program(stop=None, dump=()):
    nc = bass.Bass("TRN2", target_bir_lowering=False)
    P = Prog()
    try:
        return _build(nc, P, stop, dump)
    except _Stop:
        return nc


def _build(nc, P, stop, dump):

    def din(name, shape):
        return nc.dram_tensor(name, list(shape), F32, kind="ExternalInput").ap()

    x_d = din("x", [T, D])
    xh_d = din("xh", [128, D])
    w_in_d = din("w_in", [D, 3328])
    pool_w_d = din("pool_w", [4, 128, 128])
    p_pool_d = din("p_pool", [512, D])
    p_attn_d = din("p_attn", [512, D])
    w_out_d = din("w_out", [D, D])
    w_up_d = din("w_up", [D, DFF])
    w_dn_d = din("w_down", [DFF, D])
    g1_d = din("g1t", [128, 8])
    g2_d = din("g2t", [128, 8])
    gf_d = din("gfb", [128, D])
    pb_d = din("pbt", [128, 4])
    psc_d = din("psct", [128, 4])
    sink_d = din("sinkb", [128, 8])
    ident_d = din("ident", [128, 128])
    dm_d = din("dm", [128, 256])
    dm0_d = din("dm0", [128, 256])
    invc_d = din("invc", [128, 64])
    y_d = nc.dram_tensor("y", [T, D], F32, kind="ExternalOutput").ap()

    stack = ExitStack()
    arena = stack.enter_context(nc.sbuf_tensor("arena", [128, ARENA_F32], F32))
    psum = stack.enter_context(nc.psum_tensor("psum", [128, 4096], F32))

    dbg_d = None
    if dump:
        tot = sum(n for _, n in dump) // 4
        dbg_d = nc.dram_tensor("dbg", [128, tot], F32, kind="ExternalOutput").ap()

    def finish(slots=()):
        slots = list(slots)
        o = 0
        for i, (off, n) in enumerate(dump):
            src = arena[:, off // 4:(off + n) // 4]
            P.op("sp", lambda e, src=src, o=o, n=n: e.dma_start(out=dbg_d[:, o:o + n // 4], in_=src),
                 r=[src], dma_slot="dbg%d" % i)
            slots.append("dbg%d" % i)
            o += n // 4
        P.emit(nc, stack, slots)
        stack.close()

    def phase_end(name):
        if stop == name:
            finish()
            raise _Stop()

    def sb(off_bytes, nelem, dt):
        assert off_bytes % 4 == 0
        nb = nelem * _esize(dt)
        assert nb % 4 == 0 and off_bytes + nb <= ARENA_F32 * 4, (off_bytes, nb)
        v = arena[:, off_bytes // 4:(off_bytes + nb) // 4]
        if dt != F32:
            v = v.bitcast(dt)
        return v

    def bank(b):
        return psum[:, b * 512:(b + 1) * 512]

    def bank_bf(b):
        return psum[:, b * 512:(b + 1) * 512].bitcast(BF16)

    H0 = 0
    UT0 = 65536
    PMT0 = UT0 + 34816
    QT0 = PMT0 + 16384
    MIX0 = QT0 + 16384
    KT0 = MIX0 + 32768
    KS0 = KT0 + 4352
    V0 = KS0 + 4352
    WOUT0 = V0 + 4352
    C0 = WOUT0 + 16384

    Hbuf = sb(H0, 16 * 1024, F32).rearrange("p (t d) -> p t d", t=16)
    UT = sb(UT0, 17 * 8 * 128, BF16).rearrange("p (t c k) -> p t c k", t=17, c=8)
    PMT = sb(PMT0, 4 * 2048, BF16).rearrange("p (g n) -> p g n", g=4)
    QT = sb(QT0, 4 * 2048, BF16).rearrange("p (g n) -> p g n", g=4)
    MIXT = sb(MIX0, 16 * 8 * 128, BF16).rearrange("p (t c k) -> p t c k", t=16, c=8)
    KT = sb(KT0, 2176, BF16)
    KS = sb(KS0, 2176, BF16)
    V = sb(V0, 17 * 128, BF16).rearrange("p (t k) -> p t k", t=17)
    WOUT = sb(WOUT0, 8 * 1024, BF16).rearrange("p (c n) -> p c n", c=8)

    c = C0
    IDENT = sb(c, 128, BF16); c += 256
    DM = sb(c, 256, F32); c += 1024
    DMZ = sb(c, 256, F32); c += 1024
    G1 = sb(c, 8, F32); c += 32
    G2 = sb(c, 8, F32); c += 32
    PB = sb(c, 4, F32); c += 16
    PSC = sb(c, 4, F32); c += 16
    SINK = sb(c, 8, F32); c += 32
    INVC = sb(c, 64, F32).rearrange("p (g k) -> p g k", g=4); c += 256
    PW = sb(c, 4 * 128, BF16).rearrange("p (g k) -> p g k", g=4); c += 1024
    STAT = sb(c, 512, F32); c += 2048
    assert c <= 201216, c

    XS = [sb(MIX0 + 8192 + i * 4096, 1024, F32) for i in range(3)]
    RL = [sb(201216 + i * 2048, 512, F32) for i in range(2)]
    XB = [sb(205312 + i * 2048, 1024, BF16) for i in range(2)]
    SQJ = sb(209408, 1024, BF16)
    UP = [sb(H0 + i * 8704, 2176, F32) for i in range(2)]
    SS = [sb(H0 + 17408 + i * 8704, 2176, F32) for i in range(2)]
    DT = [sb(H0 + 34816 + i * 4096, 2048, BF16) for i in range(2)]
    WS = [sb(H0 + 49152 + i * 8192, 8 * 512, BF16).rearrange("p (c n) -> p c n", c=8) for i in range(2)]
    ASB = [sb(H0 + i * 16384, 2048, F32).rearrange("p (h k) -> p h k", h=8) for i in range(2)]
    APN = [sb(H0 + i * 16384 + 8192, 2048, BF16).rearrange("p (h k) -> p h k", h=8) for i in range(2)]
    APT = [sb(H0 + i * 16384 + 12288, 2048, BF16).rearrange("p (h k) -> p h k", h=8) for i in range(2)]
    TG = [[sb(H0 + s * 8192 + j * 2048, 512, F32) for j in range(4)] for s in range(2)]
    WG = []
    for s in range(2):
        base = H0 + 32768 + s * 12288
        WG.append(dict(
            gp=sb(base, 8 * 256, BF16).rearrange("p (c n) -> p c n", c=8),
            ga=sb(base + 4096, 8 * 256, BF16).rearrange("p (c n) -> p c n", c=8),
            pp=sb(base + 8192, 4 * 256, BF16).rearrange("p (c n) -> p c n", c=4),
            pa=sb(base + 10240, 4 * 256, BF16).rearrange("p (c n) -> p c n", c=4),
        ))
    STG = []
    for base in (PMT0, MIX0):
        STG.append(dict(
            up=sb(base, 8 * 1024, BF16).rearrange("p (c n) -> p c n", c=8),
            dn=sb(base + 16384, 8 * 1024, BF16).rearrange("p (c n) -> p c n", c=8),
        ))
    A2T = [sb(KT0 + i * 8192, 8 * 512, BF16).rearrange("p (c n) -> p c n", c=8) for i in range(2)]
    GF = sb(WOUT0 + 4096, 1024, F32)
    OUTT = [sb(WOUT0 + 8192 + i * 4096, 1024, F32) for i in range(2)]

    st_ss = [STAT[:, i:i + 1] for i in range(4)]
    st_rs = [STAT[:, 4 + i:5 + i] for i in range(4)]
    NHALF = STAT[:, 8:9]
    st_fix = STAT[:, 16:48]

    def _stset(j):
        return [STAT[:, 64 + 64 * i + 8 * j:72 + 64 * i + 8 * j] for i in range(4)]
    st_mx, st_m8, st_ng, st_rsum, st_es, st_den, st_rd = (_stset(j) for j in range(7))
    ON = [sb(H0 + 57344 + i * 1024, 512, BF16) for i in range(2)]
    MT = sb(H0 + 59392, 1024, F32).rearrange("p (h k) -> p h k", h=8)
    MT2 = sb(H0 + 63488, 512, F32).rearrange("p (h k) -> p h k", h=8)
    P.op("dve", lambda e: e.memset(NHALF, -0.5), w=[NHALF])
    PBS = STAT[:, 48:52]
    NEG1 = STAT[:, 56:64]
    P.op("dve", lambda e: e.memset(NEG1, -1.0), w=[NEG1])

    def isap(v):
        return not isinstance(v, (int, float)) and v is not None

    def mm(out, lhsT, rhs, start=True, stop=True):
        P.op("pe", lambda e: e.matmul(out, lhsT, rhs, start=start, stop=stop),
             r=[lhsT, rhs], w=[out])

    def tp(out, in_):
        P.op("pe", lambda e: e.transpose(out, in_, IDENT), r=[in_, IDENT], w=[out])

    def act(out, in_, func, bias=0.0, scale=1.0, accum=None):
        def fn(e):
            kw = {}
            if accum is not None:
                kw["accum_out"] = accum
            return e.activation(out, in_, func, bias=bias, scale=scale, **kw)
        P.op("act", fn, r=[in_] + [v for v in (bias, scale) if isap(v)], w=[out, accum])

    def tt(eng, out, in0, in1, op):
        P.op(eng, lambda e: e.tensor_tensor(out, in0, in1, op), r=[in0, in1], w=[out])

    def ts(eng, out, in0, s1, s2, op0, op1=None):
        def fn(e):
            if op1 is None:
                return e.tensor_scalar(out, in0, s1, None, op0)
            return e.tensor_scalar(out, in0, s1, s2, op0, op1)
        P.op(eng, fn, r=[in0] + [v for v in (s1, s2) if isap(v)], w=[out])

    def stt(eng, out, in0, scalar, in1, op0, op1):
        P.op(eng, lambda e: e.scalar_tensor_tensor(out, in0, scalar, in1, op0, op1),
             r=[in0, in1] + ([scalar] if isap(scalar) else []), w=[out])

    def cp(eng, out, in_):
        if eng == "act":
            P.op("act", lambda e: e.copy(out, in_), r=[in_], w=[out])
        else:
            P.op(eng, lambda e: e.tensor_copy(out, in_), r=[in_], w=[out])

    def dma(q, out, in_, slot):
        P.op(q, lambda e: e.dma_start(out=out, in_=in_), r=[in_], w=[out], dma_slot=slot)

    dma("pool", IDENT, ident_d, "ident")
    def load_small_consts():
        dma("sp", G1, g1_d, "g1")
        dma("sp", PB, pb_d, "pb")
        dma("sp", PSC, psc_d, "psc")
        dma("sp", INVC, invc_d.rearrange("p (g k) -> p g k", g=4), "invc")
        dma("sp", DM, dm_d, "dm")
        dma("sp", DMZ, dm0_d, "dm0")
        dma("sp", SINK, sink_d, "sink")
        dma("sp", G2, g2_d, "g2")
        tt("dve", PBS, PB, PSC, ALU.mult)

    w_in_v = w_in_d.rearrange("(c p) n -> p c n", p=128)
    for g in range(4):
        dma("pool", WS[0][:, :, g * 128:(g + 1) * 128], w_in_v[:, :, g * 128:(g + 1) * 128], "ws0_%d" % g)
    dma("pool", WS[1][:, :, 0:256], w_in_v[:, :, 1024:1280], "ws1")
    dma("pool", PW, pool_w_d.rearrange("g c d -> c g d"), "pw")

    WS2 = sb(MIX0, 8 * 512, BF16).rearrange("p (c n) -> p c n", c=8)
    dma("pool", WS2, w_in_v[:, :, 512:1024], "ws2")

    def norm_stats(src, k):
        ss, rs = st_ss[k % 4], st_rs[k % 4]
        act(SQJ, src, AF.Square, scale=1.0 / 32.0, accum=ss)
        ts("dve", ss, ss, EPS, None, ALU.add)
        tt("pool", rs, ss, NHALF, ALU.pow)
        return rs

    def norm_transpose(xb, tile_idx, gain, pbank):
        pt = bank_bf(pbank).rearrange("p (c k) -> p c k", c=8)
        for cc in range(8):
            tp(pt[:, cc, :], xb[:, cc * 128:(cc + 1) * 128])
        gb = gain.unsqueeze(2).to_broadcast([128, 8, 128])
        tt("dve", UT[:, tile_idx, :, :], pt, gb, ALU.mult)

    nb = [0]

    def next_bank():
        b = 2 + nb[0] % 6
        nb[0] += 1
        return b

    PIECES = [(0, 128)] + [(128 + 512 * s, 512) for s in range(4)]

    def ut_cols(c0, n, dchunk):
        t0 = c0 // 128
        return UT[:, t0:t0 + n // 128, dchunk, :]

    def evac_copy(dst, src):
        cp("act", dst, src)

    def evac_q(dst, src):
        act(dst, src, AF.Copy, scale=0.125)

    def piece_job(wbuf, wc0, dst_fn, evac_fn, piece, after=None):
        c0, n = piece

        def fn():
            b = next_bank()
            o = bank(b)[:, 0:n]
            for dchunk in range(8):
                mm(o, wbuf[:, dchunk, wc0:wc0 + 128], ut_cols(c0, n, dchunk),
                   start=(dchunk == 0), stop=(dchunk == 7))
            evac_fn(dst_fn(c0, n), o)
            if after is not None:
                after()
        return ((c0 + n) // 128 - 1, fn)

    def pool_ops(g):
        up, d_t = UP[g % 2], DT[g % 2]
        w = POOLW[g]
        eng = "pool" if g < 3 else "dve"
        cur = up
        sh = 1
        k = 0
        while sh < w:
            nxt = SS[k % 2]
            lo = 2 * sh - 1
            tt(eng, nxt[:, lo:2176], cur[:, lo:2176], cur[:, lo - sh:2176 - sh], ALU.add)
            cur = nxt
            sh *= 2
            k += 1
        stt("dve", d_t[:, 0:2048], cur[:, 128:2176], 1.0 / w, up[:, 128:2176], ALU.mult, ALU.subtract)
        fx = st_fix[:, 0:16]
        tt("dve", fx, cur[:, 128:144], INVC[:, g, :], ALU.mult)
        tt("dve", d_t[:, 0:16], fx, up[:, 128:144], ALU.subtract)

    def pool_linear(g):
        d_t = DT[g % 2]
        for s in range(4):
            b = next_bank()
            o = bank(b)
            mm(o, PW[:, g, :], d_t[:, s * 512:(s + 1) * 512])
            act(PMT[:, g, s * 512:(s + 1) * 512], o, AF.Identity, bias=PBS[:, g:g + 1],
                scale=PSC[:, g:g + 1])

    VT = sb(H0 + 43008, 2176, BF16)

    def v_transposes():
        for t0 in range(0, 17, 4):
            nt = min(4, 17 - t0)
            b = next_bank()
            pt = bank_bf(b)[:, 0:nt * 128].rearrange("p (t k) -> p t k", t=nt)
            for j in range(nt):
                tp(pt[:, j, :], VT[:, (t0 + j) * 128:(t0 + j + 1) * 128])
            evac_copy(V[:, t0:t0 + nt, :], pt)

    def v_chunk(p):
        return piece_job(WS[1], 128, lambda c0, n: VT[:, c0:c0 + n], evac_copy, PIECES[p],
                         after=v_transposes if p == 4 else None)

    def v_job(t0):
        nt = min(4, 17 - t0)

        def fn():
            b = next_bank()
            o = bank(b)
            for j in range(nt):
                for dchunk in range(8):
                    mm(o[:, j * 128:(j + 1) * 128], UT[:, t0 + j, dchunk, :], WS[1][:, dchunk, 128:256],
                       start=(dchunk == 0), stop=(dchunk == 7))
            evac_copy(V[:, t0:t0 + nt, :], o[:, 0:nt * 128].rearrange("p (t k) -> p t k", t=nt))
        return (t0 + nt - 1, fn)

    def ks_swap():
        dma("sp", KS[64:128, :], KT[0:64, :], "ksa")
        dma("sp", KS[0:64, :], KT[64:128, :], "ksb")

    def pool_chunk(g, p):
        return piece_job(WS[0], g * 128, lambda c0, n, g=g: UP[g % 2][:, c0:c0 + n], evac_copy,
                         PIECES[p], after=(lambda g=g: pool_ops(g)) if p == 4 else None)

    def k_chunk(p):
        return piece_job(WS[1], 0, lambda c0, n: KT[:, c0:c0 + n], evac_copy, PIECES[p],
                         after=ks_swap if p == 4 else None)

    def q_chunk(qc, p):
        return piece_job(WS2, qc * 128, lambda c0, n, qc=qc: QT[:, qc, c0 - 128:c0 - 128 + n],
                         evac_q, PIECES[p])

    jobs = []
    for p in range(5):
        jobs.append(pool_chunk(0, p))
        jobs.append(k_chunk(p))
        jobs.append(v_chunk(p))
        jobs.append(pool_chunk(1, p))
        if p >= 1:
            jobs.append(q_chunk(0, p))
            jobs.append(q_chunk(1, p))
    for p in range(4):
        jobs.append(pool_chunk(2, p))
    jobs.append((16, lambda: pool_linear(0)))
    jobs.append(pool_chunk(2, 4))
    for p in range(4):
        jobs.append(pool_chunk(3, p))
    jobs.append((16, lambda: pool_linear(1)))
    jobs.append(pool_chunk(3, 4))
    for p in range(1, 5):
        jobs.append(q_chunk(2, p))
    jobs.append((16, lambda: pool_linear(2)))
    for p in range(1, 5):
        jobs.append(q_chunk(3, p))
    jobs.append((16, lambda: pool_linear(3)))

    ji = 0
    for t in range(19):
        if t < 17:
            xs = XS[t % 3]
            src = xh_d if t == 0 else x_d[(t - 1) * 128:t * 128, :]
            dma("sp", xs, src, "xs%d" % (t % 3))
            act(SQJ, xs, AF.Square, scale=1.0 / 32.0, accum=st_ss[t % 4])
        if t == 2:
            load_small_consts()
        if t >= 2:
            u = t - 2
            norm_transpose(XB[u % 2], u, G1, u % 2)
        if 1 <= t <= 17:
            u = t - 1
            ts("dve", XB[u % 2], XS[u % 3], st_rs[u % 4], None, ALU.mult)
        if t < 17:
            ts("dve", st_ss[t % 4], st_ss[t % 4], EPS, None, ALU.add)
            tt("pool", st_rs[t % 4], st_ss[t % 4], NHALF, ALU.pow)
        if t >= 2:
            if ji < len(jobs) and jobs[ji][0] <= t - 2:
                jobs[ji][1]()
                ji += 1
    phase_end('A')
    while ji < len(jobs):
        jobs[ji][1]()
        ji += 1

    phase_end('C')
    p_pool_v = p_pool_d.rearrange("(c p) n -> p c n", p=128)
    p_attn_v = p_attn_d.rearrange("(c p) n -> p c n", p=128)

    def load_pair(pr):
        wg = WG[pr % 2]
        s = "wg%d" % (pr % 2)
        dma("pool", wg["gp"], w_in_v[:, :, 1280 + pr * 256:1280 + (pr + 1) * 256], s + "gp")
        dma("pool", wg["ga"], w_in_v[:, :, 2304 + pr * 256:2304 + (pr + 1) * 256], s + "ga")
        dma("pool", wg["pp"], p_pool_v[:, :, pr * 256:(pr + 1) * 256], s + "pp")
        dma("pool", wg["pa"], p_attn_v[:, :, pr * 256:(pr + 1) * 256], s + "pa")

    load_pair(0)
    load_pair(1)
    dma("pool", WOUT, w_out_d.rearrange("(c p) n -> p c n", p=128), "wout")

    BIAS = sb(MIX0, 2048, F32).rearrange("p (h k) -> p h k", h=8)
    BIASZ = sb(MIX0 + 20480, 2048, F32).rearrange("p (h k) -> p h k", h=8)
    if FUSED_MAX or BIAS_TT:
        for h in range(8):
            ts("dve", BIAS[:, h, :], DM, -SLOPES[h], None, ALU.mult)
            ts("dve", BIASZ[:, h, :], DMZ, -SLOPES[h], None, ALU.mult)

    def att_s1(b):
        asb = ASB[b % 2]
        dmt = DMZ if b == 0 else DM
        for h in range(8):
            cc, r, g = h // 2, h % 2, h // 4
            ksrc = KT if r == g else KS
            o = bank(h % 4)[:, (h // 4) * 256:(h // 4) * 256 + 256]
            mm(o, QT[r * 64:(r + 1) * 64, cc, b * 128:(b + 1) * 128],
               ksrc[r * 64:(r + 1) * 64, b * 128:b * 128 + 256])
        mx = st_mx[b % 4]
        if FUSED_MAX:
            bt = BIASZ if b == 0 else BIAS
            for h in range(8):
                o = bank(h % 4)[:, (h // 4) * 256:(h // 4) * 256 + 256]
                P.op("dve", lambda e, h=h, o=o: e.tensor_tensor_reduce(
                    out=asb[:, h, :], in0=o, in1=bt[:, h, :], scale=1.0, scalar=-3.0e38,
                    op0=ALU.add, op1=ALU.max, accum_out=mx[:, h:h + 1]),
                    r=[o, bt[:, h, :]], w=[asb[:, h, :], mx[:, h:h + 1]])
        elif BIAS_TT:
            bt = BIASZ if b == 0 else BIAS
            for j in range(4):
                tt("dve", asb[:, j:8:4, :], bank(j).rearrange("p (h k) -> p h k", h=2), bt[:, j:8:4, :],
                   ALU.add)
            P.op("dve", lambda e: e.reduce_max(mx, asb, AX.X), r=[asb], w=[mx])
        else:
            for h in range(8):
                o = bank(h % 4)[:, (h // 4) * 256:(h // 4) * 256 + 256]
                stt("dve", asb[:, h, :], dmt, -SLOPES[h], o, ALU.mult, ALU.add)
            if TS_MAX:
                for h in range(8):
                    P.op("dve", lambda e, h=h: e.tensor_scalar(asb[:, h, :], asb[:, h, :], 1.0, None,
                                                              ALU.mult, ALU.max, accum_out=mx[:, h:h + 1]),
                         r=[asb[:, h, :]], w=[asb[:, h, :], mx[:, h:h + 1]])
            else:
                P.op("dve", lambda e: e.reduce_max(mx, asb, AX.X), r=[asb], w=[mx])

    def att_tiny(b):
        k4 = b % 4
        mx, m8, ng, es = st_mx[k4], st_m8[k4], st_ng[k4], st_es[k4]
        tt("dve", m8, mx, SINK, ALU.max)
        tt("pool", ng, m8, NEG1, ALU.mult)
        tt("pool", es, SINK, m8, ALU.subtract)

    def att_exp(b):
        k4 = b % 4
        asb, apb = ASB[b % 2], APN[b % 2]
        ng, rsum, es = st_ng[k4], st_rsum[k4], st_es[k4]
        if POOL_ROWSUM:
            for h in range(8):
                act(apb[:, h, :], asb[:, h, :], AF.Exp, bias=ng[:, h:h + 1])
            act(es, es, AF.Exp)
            tt("pool", MT, apb[:, :, 0:128], apb[:, :, 128:256], ALU.add)
            tt("pool", MT2, MT[:, :, 0:64], MT[:, :, 64:128], ALU.add)
            cur, w, k = MT2, 64, 0
            while w > 2:
                nxt = MT if k % 2 == 0 else MT2
                tt("pool", nxt[:, :, 0:w // 2], cur[:, :, 0:w // 2], cur[:, :, w // 2:w], ALU.add)
                cur, w, k = nxt, w // 2, k + 1
            tt("pool", rsum.unsqueeze(2), cur[:, :, 0:1], cur[:, :, 1:2], ALU.add)
        else:
            for h in range(8):
                act(apb[:, h, :], asb[:, h, :], AF.Exp, bias=ng[:, h:h + 1], accum=rsum[:, h:h + 1])
            act(es, es, AF.Exp)

    def att_tr(b):
        apb = APN[b % 2]
        for half in range(2):
            pt = bank_bf(4 + half).rearrange("p (h k) -> p h k", h=4)
            for hh in range(4):
                h = half * 4 + hh
                for kc in range(2):
                    tp(pt[:, hh, kc * 128:(kc + 1) * 128], apb[:, h, kc * 128:(kc + 1) * 128])

    def att_ptevac(b):
        apt = APT[b % 2]
        for half in range(2):
            pt = bank_bf(4 + half).rearrange("p (h k) -> p h k", h=4)
            cp("dve" if (FUSED_MAX and half == 1) else "act", apt[:, half * 4:half * 4 + 4, :], pt)

    def att_pv(b):
        k4 = b % 4
        apt = APT[b % 2]
        rsum, es, den, rd = st_rsum[k4], st_es[k4], st_den[k4], st_rd[k4]
        ob = bank(6)
        for h in range(8):
            g = h // 4
            for kc in range(2):
                mm(ob[:, h * 64:(h + 1) * 64], apt[:, h, kc * 128:(kc + 1) * 128],
                   V[:, b + kc, g * 64:(g + 1) * 64], start=(kc == 0), stop=(kc == 1))
        tt("pool", den, rsum, es, ALU.add)
        P.op("dve", lambda e: e.reciprocal(rd, den), r=[den], w=[rd])

    def att_s4_on(b):
        on, rd = ON[b % 2], st_rd[b % 4]
        ob = bank(6)
        tt("dve", on.rearrange("p (h d) -> p h d", h=8), ob.rearrange("p (h d) -> p h d", h=8),
           rd.unsqueeze(2).to_broadcast([128, 8, 64]), ALU.mult)

    def att_s4_t(b):
        on = ON[b % 2]
        pt7 = bank_bf(7)[:, 0:512].rearrange("p (c k) -> p c k", c=4)
        for cc in range(4):
            tp(pt7[:, cc, :], on[:, cc * 128:(cc + 1) * 128])

    def att_s4_evac(b):
        pt7 = bank_bf(7)[:, 0:512].rearrange("p (c k) -> p c k", c=4)
        cp("act" if POOL_ROWSUM else "dve", QT[:, :, b * 128:(b + 1) * 128], pt7)

    def blk(j):
        return 0 <= j < 16

    for i in range(20):
        if blk(i - 1):
            att_tiny(i - 1)
        if blk(i - 2):
            att_ptevac(i - 2)
        if blk(i - 1):
            att_exp(i - 1)
        if blk(i - 3):
            att_s4_on(i - 3)
        if blk(i):
            att_s1(i)
        if blk(i - 3):
            att_s4_t(i - 3)
        if blk(i - 2):
            att_pv(i - 2)
        if blk(i - 1):
            att_tr(i - 1)
        if blk(i - 3):
            att_s4_evac(i - 3)
    OT = QT
    phase_end('D')

    for pr in range(4):
        wg = WG[pr % 2]
        for mmi in range(2):
            m = pr * 2 + mmi
            wc = mmi * 128
            for s in range(4):
                st = (m * 4 + s) % 2
                bgp, bga, byp, bya = (bank(st * 4 + i) for i in range(4))
                for dchunk in range(8):
                    mm(bgp, wg["gp"][:, dchunk, wc:wc + 128], UT[:, 1 + 4 * s:5 + 4 * s, dchunk, :],
                       start=(dchunk == 0), stop=(dchunk == 7))
                for dchunk in range(8):
                    mm(bga, wg["ga"][:, dchunk, wc:wc + 128], UT[:, 1 + 4 * s:5 + 4 * s, dchunk, :],
                       start=(dchunk == 0), stop=(dchunk == 7))
                for j in range(4):
                    mm(byp, wg["pp"][:, j, wc:wc + 128], PMT[:, j, s * 512:(s + 1) * 512],
                       start=(j == 0), stop=(j == 3))
                for j in range(4):
                    mm(bya, wg["pa"][:, j, wc:wc + 128], OT[:, j, s * 512:(s + 1) * 512],
                       start=(j == 0), stop=(j == 3))
                tg = TG[st]
                act(tg[0], bgp, AF.Tanh, scale=0.5)
                act(tg[1], bga, AF.Tanh, scale=0.5)
                stt("dve", tg[2], tg[0], 1.0, byp, ALU.add, ALU.mult)
                stt("dve", tg[3], tg[1], 1.0, bya, ALU.add, ALU.mult)
                tt("pool", MIXT[:, 4 * s:4 * s + 4, m, :],
                   tg[2].rearrange("p (t k) -> p t k", t=4),
                   tg[3].rearrange("p (t k) -> p t k", t=4), ALU.add)
        if pr + 2 < 4:
            load_pair(pr + 2)

    phase_end('E')
    w_up_v = w_up_d.rearrange("(c p) n -> p c n", p=128)
    w_dn_v = w_dn_d.rearrange("(c p) n -> p c n", p=128)

    def load_stage(sg):
        buf = STG[sg % 2]
        dma("pool", buf["up"], w_up_v[:, :, sg * 1024:(sg + 1) * 1024], "stg%dup" % (sg % 2))
        dma("pool", buf["dn"], w_dn_v[:, sg * 8:(sg + 1) * 8, :], "stg%ddn" % (sg % 2))

    load_stage(0)

    rs_f = {}
    for t in range(17):
        if t < 16:
            dma("sp", Hbuf[:, t, :], x_d[t * 128:(t + 1) * 128, :], "h%d" % t)
            for hh in range(2):
                o = bank(2 * (t % 3) + hh)
                for j in range(8):
                    mm(o, MIXT[:, t, j, :], WOUT[:, j, hh * 512:(hh + 1) * 512],
                       start=(j == 0), stop=(j == 7))
                hv = Hbuf[:, t, hh * 512:(hh + 1) * 512]
                stt("dve", hv, o, 0.5, hv, ALU.mult, ALU.add)
        if t >= 1:
            u = t - 1
            act(XB[u % 2], Hbuf[:, u, :], AF.Copy, scale=rs_f[u])
            norm_transpose(XB[u % 2], u + 1, G2, u % 2 + 6)
        if t < 16:
            rs_f[t] = norm_stats(Hbuf[:, t, :], t)
    phase_end('F')
    load_stage(1)
    dma("sp", GF, gf_d, "gf")

    y_v = y_d.rearrange("(t p) d -> t p d", p=128)
    out_slots = []

    def mlp_up(sg, s):
        buf = STG[sg % 2]
        a2 = A2T[(sg * 4 + s) % 2]
        for fc in range(8):
            o = bank(fc % 4)
            for dchunk in range(8):
                mm(o, buf["up"][:, dchunk, fc * 128:(fc + 1) * 128],
                   UT[:, 1 + 4 * s:5 + 4 * s, dchunk, :], start=(dchunk == 0), stop=(dchunk == 7))
            rl = RL[fc % 2]
            act(rl, o, AF.Relu)
            tt("pool" if fc % 2 else "dve", a2[:, fc, :], rl, rl, ALU.mult)

    pending = []

    def emit_final(t, rs):
        ot = OUTT[t % 2]
        stt("dve", ot, Hbuf[:, t, :], rs, GF, ALU.mult, ALU.mult)
        slot = "out%d" % (t % 2)
        dma("sp", y_v[t], ot, slot)
        if slot not in out_slots:
            out_slots.append(slot)

    def mlp_down(sg, s):
        buf = STG[sg % 2]
        a2 = A2T[(sg * 4 + s) % 2]
        k = 0
        for ti in range(4):
            t = 4 * s + ti
            for hh in range(2):
                o = bank(4 + k % 4)
                k += 1
                for fc in range(8):
                    mm(o, a2[:, fc, ti * 128:(ti + 1) * 128], buf["dn"][:, fc, hh * 512:(hh + 1) * 512],
                       start=(fc == 0), stop=(fc == 7))
                hv = Hbuf[:, t, hh * 512:(hh + 1) * 512]
                tt("dve", hv, o, hv, ALU.add)
            if sg == 3:
                rs = norm_stats(Hbuf[:, t, :], t)
                if pending:
                    emit_final(*pending.pop())
                pending.append((t, rs))

    seq = [(sg, s) for sg in range(4) for s in range(4)]
    for i in range(len(seq) + 1):
        if i < len(seq):
            mlp_up(*seq[i])
        if i >= 1:
            sg, s = seq[i - 1]
            mlp_down(sg, s)
            if s == 3 and sg + 2 < 4:
                load_stage(sg + 2)

    while pending:
        emit_final(*pending.pop())
    finish(out_slots)
    return nc


_CACHE = {}


def _consts():
    qi = np.arange(128)[:, None]
    kj = np.arange(256)[None, :]
    dist = (128 + qi - kj).astype(np.float32)
    valid = (dist >= 0) & (dist < 128)
    dm = np.where(valid, dist, np.float32(1e32)).astype(np.float32)
    dm0 = dm.copy()
    dm0[:, :128] = np.float32(1e32)
    ident = np.eye(128, dtype=np.float32)
    inv_mid = np.zeros((128, 4, 16), np.float32)
    inv_first = np.zeros((128, 4, 16), np.float32)
    for g, w in enumerate(POOLW):
        inv_mid[:, g, :] = np.float32(1.0) / np.float32(w)
        cnt = np.minimum(np.arange(16) + 1, w).astype(np.float32)
        inv_first[:, g, :] = (np.float32(1.0) / cnt)[None, :]
    return dm, dm0, ident, inv_mid.reshape(128, 64), inv_first.reshape(128, 64)


def kernel(x, norm_mix, w_in, pool_w, pool_b, pool_scale, attn_sinks, p_pool, p_attn,
           w_out, norm_mlp, w_up, w_down, norm_final):
    if "nc" not in _CACHE:
        _CACHE["nc"] = build_program()
    nc = _CACHE["nc"]
    in_maps = make_in_maps(x, norm_mix, w_in, pool_w, pool_b, pool_scale, attn_sinks, p_pool, p_attn,
                           w_out, norm_mlp, w_up, w_down, norm_final)
    res = run_bass_kernel_spmd(nc, in_maps, core_ids=list(range(NCORES)))
    out = np.empty((2, 4 * T, D), np.float32)
    for i in range(NCORES):
        b, ch = i // 4, i % 4
        out[b, ch * T:(ch + 1) * T] = np.asarray(res.results[i]["y"], dtype=np.float32).reshape(T, D)
    return out


def make_in_maps(x, norm_mix, w_in, pool_w, pool_b, pool_scale, attn_sinks, p_pool, p_attn,
                 w_out, norm_mlp, w_up, w_down, norm_final):
    f = lambda a: np.ascontiguousarray(np.asarray(a, dtype=np.float32))
    x = f(x)
    dm, dm0, ident, inv_mid, inv_first = _consts()
    shared = {
        "w_in": f(w_in[0]), "pool_w": f(pool_w[0]), "p_pool": f(p_pool[0]), "p_attn": f(p_attn[0]),
        "w_out": f(w_out[0]), "w_up": f(w_up[0]), "w_down": f(w_down[0]),
        "g1t": f(np.asarray(norm_mix[0]).reshape(8, 128).T),
        "g2t": f(np.asarray(norm_mlp[0]).reshape(8, 128).T),
        "gfb": f(np.broadcast_to(np.asarray(norm_final)[None, :], (128, D))),
        "pbt": f(np.asarray(pool_b[0]).reshape(4, 128).T),
        "psct": f(np.asarray(pool_scale[0]).reshape(4, 128).T),
        "sinkb": f(np.broadcast_to(np.asarray(attn_sinks[0])[None, :], (128, 8))),
        "ident": ident, "dm": dm,
    }
    in_maps = []
    for i in range(NCORES):
        b, ch = i // 4, i % 4
        m = dict(shared)
        m["x"] = f(x[b, ch * T:(ch + 1) * T])
        if ch == 0:
            m["xh"] = np.zeros((128, D), np.float32)
            m["dm0"] = dm0
            m["invc"] = inv_first
        else:
            m["xh"] = f(x[b, ch * T - 128:ch * T])
            m["dm0"] = dm
            m["invc"] = inv_mid
        in_maps.append(m)
    return in_maps
```

```python
import numpy as np
from contextlib import ExitStack

import concourse.bass as bass
import concourse.mybir as mybir
from concourse.bass_utils import run_bass_kernel_spmd

F32 = mybir.dt.float32
BF16 = mybir.dt.bfloat16
ALU = mybir.AluOpType
AF = mybir.ActivationFunctionType
AX = mybir.AxisListType

NCORES = 8
D = 1024
T = 2048
NT = 16
DFF = 4096
EPS = 1e-5
SLOPES = [2.0 ** (-(h + 1)) for h in range(8)]
POOLW = (2, 4, 8, 16)

ARENA_F32 = 53000
BUCKET = 1024
BIAS_TT = True
TS_MAX = False
FUSED_MAX = False


def _esize(dt):
    if dt == F32:
        return 4
    if dt == BF16:
        return 2
    raise ValueError(str(dt))


class Foot:
    __slots__ = ("space", "p0", "p1", "ivs")

    def __init__(self, space, p0, p1, ivs):
        self.space, self.p0, self.p1, self.ivs = space, p0, p1, ivs


def footprint(ap):
    t = ap.tensor
    tn = type(t).__name__
    if tn.startswith("DRam"):
        return None
    space = "ps" if tn.startswith("PSum") else "sb"
    es = _esize(ap.dtype)
    dims = [(int(s), int(c)) for (s, c) in ap.ap]
    off = int(ap.offset)
    pstep, pcnt = dims[0]
    assert pstep > 0
    p0 = off // pstep
    col = off % pstep
    free = dims[1:]
    if space == "ps":
        lo = col
        hi = col + sum((c - 1) * abs(s) for s, c in free) + 1
        b0 = (lo * es) // 2048
        b1 = (hi * es - 1) // 2048
        return Foot("ps", 0, 128, [(b0 * 2048, (b1 + 1) * 2048)])
    outer = 1
    for s, c in free[:-1]:
        outer *= c
    if len(free) == 0:
        ivs = [(col, col + 1)]
    elif outer > 512:
        hi = col + sum((c - 1) * abs(s) for s, c in free) + 1
        ivs = [(col, hi)]
    else:
        starts = [col]
        for s, c in free[:-1]:
            starts = [st + s * k for st in starts for k in range(c)]
        s, c = free[-1]
        ln = (c - 1) * abs(s) + 1
        ivs = [(st, st + ln) for st in starts]
    ivs = sorted((a * es, b * es) for a, b in ivs)
    merged = []
    for a, b in ivs:
        if merged and a <= merged[-1][1]:
            merged[-1] = (merged[-1][0], max(b, merged[-1][1]))
        else:
            merged.append((a, b))
    return Foot("sb", p0, p0 + pcnt, merged)


class Op:
    __slots__ = ("eng", "fn", "idx", "deps", "signal", "dma_slot", "dma_k", "sigval")

    def __init__(self, eng, fn):
        self.eng, self.fn = eng, fn
        self.deps = {}
        self.signal = False
        self.dma_slot = None
        self.dma_k = 0
        self.sigval = 0


class Prog:
    def __init__(self):
        self.ops = []
        self.per_eng = {"pe": [], "dve": [], "act": [], "pool": [], "sp": []}
        self.tr = {"sb": {}, "ps": {}}
        self.dma_count = {}

    def _conflicts(self, op, foot, is_write):
        tr = self.tr[foot.space]
        for (a, b) in foot.ivs:
            for bk in range(a // BUCKET, (b - 1) // BUCKET + 1):
                lst = tr.get(bk)
                if not lst:
                    continue
                lo = max(a, bk * BUCKET)
                hi = min(b, (bk + 1) * BUCKET)
                for e in lst:
                    if e[0] < hi and lo < e[1] and e[2] < foot.p1 and foot.p0 < e[3]:
                        if not (is_write or e[6]):
                            continue
                        key, n = e[4], e[5]
                        if op.eng == "pe" and key == "pe" and op.dma_slot is None:
                            continue
                        if op.deps.get(key, -1) < n:
                            op.deps[key] = n

    def _record(self, op, foot, is_write, key, n):
        tr = self.tr[foot.space]
        for (a, b) in foot.ivs:
            for bk in range(a // BUCKET, (b - 1) // BUCKET + 1):
                lo = max(a, bk * BUCKET)
                hi = min(b, (bk + 1) * BUCKET)
                lst = tr.setdefault(bk, [])
                if is_write:
                    lst[:] = [e for e in lst if not (lo <= e[0] and e[1] <= hi
                                                     and foot.p0 <= e[2] and e[3] <= foot.p1)]
                else:
                    lst[:] = [e for e in lst if not (e[4] == key and not e[6] and e[0] == lo
                                                     and e[1] == hi and e[2] == foot.p0
                                                     and e[3] == foot.p1)]
                lst.append([lo, hi, foot.p0, foot.p1, key, n, is_write])

    def op(self, eng, fn, r=(), w=(), dma_slot=None):
        op = Op(eng, fn)
        op.idx = len(self.ops)
        rf = [f for f in (footprint(a) for a in r if a is not None) if f is not None]
        wf = [f for f in (footprint(a) for a in w if a is not None) if f is not None]
        if dma_slot is not None:
            op.dma_slot = dma_slot
            k = self.dma_count.get(dma_slot, 0) + 1
            self.dma_count[dma_slot] = k
            op.dma_k = k
            key, n = ("dma", dma_slot), k
        else:
            key, n = eng, len(self.per_eng[eng])
        for f in rf:
            self._conflicts(op, f, f.space == "ps")
        for f in wf:
            self._conflicts(op, f, True)
        for f in rf:
            self._record(op, f, f.space == "ps", key, n)
        for f in wf:
            self._record(op, f, True, key, n)
        self.ops.append(op)
        self.per_eng[eng].append(op)
        return op

    def emit(self, nc, stack, final_waits):
        for op in self.ops:
            for key, n in op.deps.items():
                if isinstance(key, tuple):
                    continue
                self.per_eng[key][n].signal = True
        sems = {}
        for e in self.per_eng:
            sems[e] = stack.enter_context(nc.semaphore("s_" + e))
            c = 0
            for op in self.per_eng[e]:
                if op.dma_slot is None and op.signal:
                    c += 1
                    op.sigval = c
        dsem = {}
        for slot in self.dma_count:
            dsem[slot] = stack.enter_context(nc.semaphore("d_%s" % (slot,)))
        block = stack.enter_context(nc.Block())

        def stream(ename, eng):
            waited = {}
            for op in self.per_eng[ename]:
                for key, n in op.deps.items():
                    if isinstance(key, tuple):
                        sem, val = dsem[key[1]], 16 * n
                    else:
                        sem, val = sems[key], self.per_eng[key][n].sigval
                    if waited.get(key, -1) >= val:
                        continue
                    waited[key] = val
                    eng.wait_ge(sem, val)
                ins = op.fn(eng)
                if op.dma_slot is not None:
                    ins.then_inc(dsem[op.dma_slot], 16)
                elif op.signal:
                    ins.then_inc(sems[ename], 1)
            if ename == "sp":
                for slot in final_waits:
                    eng.wait_ge(dsem[slot], 16 * self.dma_count[slot])

        @block.tensor
        def _(e):
            stream("pe", e)

        @block.vector
        def _(e):
            stream("dve", e)

        @block.scalar
        def _(e):
            stream("act", e)

        @block.gpsimd
        def _(e):
            stream("pool", e)

        @block.sync
        def _(e):
            stream("sp", e)


class _Stop(Exception):
    pass


def build_program(stop=None, dump=()):
    nc = bass.Bass("TRN2", target_bir_lowering=False)
    P = Prog()
    try:
        return _build(nc, P, stop, dump)
    except _Stop:
        return nc


def _build(nc, P, stop, dump):

    def din(name, shape):
        return nc.dram_tensor(name, list(shape), F32, kind="ExternalInput").ap()

    x_d = din("x", [T, D])
    xh_d = din("xh", [128, D])
    w_in_d = din("w_in", [D, 3328])
    pool_w_d = din("pool_w", [4, 128, 128])
    p_pool_d = din("p_pool", [512, D])
    p_attn_d = din("p_attn", [512, D])
    w_out_d = din("w_out", [D, D])
    w_up_d = din("w_up", [D, DFF])
    w_dn_d = din("w_down", [DFF, D])
    g1_d = din("g1t", [128, 8])
    g2_d = din("g2t", [128, 8])
    gf_d = din("gfb", [128, D])
    pb_d = din("pbt", [128, 4])
    psc_d = din("psct", [128, 4])
    sink_d = din("sinkb", [128, 8])
    ident_d = din("ident", [128, 128])
    dm_d = din("dm", [128, 256])
    dm0_d = din("dm0", [128, 256])
    invc_d = din("invc", [128, 64])
    y_d = nc.dram_tensor("y", [T, D], F32, kind="ExternalOutput").ap()

    stack = ExitStack()
    arena = stack.enter_context(nc.sbuf_tensor("arena", [128, ARENA_F32], F32))
    psum = stack.enter_context(nc.psum_tensor("psum", [128, 4096], F32))

    dbg_d = None
    if dump:
        tot = sum(n for _, n in dump) // 4
        dbg_d = nc.dram_tensor("dbg", [128, tot], F32, kind="ExternalOutput").ap()

    def finish(slots=()):
        slots = list(slots)
        o = 0
        for i, (off, n) in enumerate(dump):
            src = arena[:, off // 4:(off + n) // 4]
            P.op("sp", lambda e, src=src, o=o, n=n: e.dma_start(out=dbg_d[:, o:o + n // 4], in_=src),
                 r=[src], dma_slot="dbg%d" % i)
            slots.append("dbg%d" % i)
            o += n // 4
        P.emit(nc, stack, slots)
        stack.close()

    def phase_end(name):
        if stop == name:
            finish()
            raise _Stop()

    def sb(off_bytes, nelem, dt):
        assert off_bytes % 4 == 0
        nb = nelem * _esize(dt)
        assert nb % 4 == 0 and off_bytes + nb <= ARENA_F32 * 4, (off_bytes, nb)
        v = arena[:, off_bytes // 4:(off_bytes + nb) // 4]
        if dt != F32:
            v = v.bitcast(dt)
        return v

    def bank(b):
        return psum[:, b * 512:(b + 1) * 512]

    def bank_bf(b):
        return psum[:, b * 512:(b + 1) * 512].bitcast(BF16)

    H0 = 0
    UT0 = 65536
    PMT0 = UT0 + 34816
    QT0 = PMT0 + 16384
    MIX0 = QT0 + 16384
    KT0 = MIX0 + 32768
    KS0 = KT0 + 4352
    V0 = KS0 + 4352
    WOUT0 = V0 + 4352
    C0 = WOUT0 + 16384

    Hbuf = sb(H0, 16 * 1024, F32).rearrange("p (t d) -> p t d", t=16)
    UT = sb(UT0, 17 * 8 * 128, BF16).rearrange("p (t c k) -> p t c k", t=17, c=8)
    PMT = sb(PMT0, 4 * 2048, BF16).rearrange("p (g n) -> p g n", g=4)
    QT = sb(QT0, 4 * 2048, BF16).rearrange("p (g n) -> p g n", g=4)
    MIXT = sb(MIX0, 16 * 8 * 128, BF16).rearrange("p (t c k) -> p t c k", t=16, c=8)
    KT = sb(KT0, 2176, BF16)
    KS = sb(KS0, 2176, BF16)
    V = sb(V0, 17 * 128, BF16).rearrange("p (t k) -> p t k", t=17)
    WOUT = sb(WOUT0, 8 * 1024, BF16).rearrange("p (c n) -> p c n", c=8)

    c = C0
    IDENT = sb(c, 128, BF16); c += 256
    DM = sb(c, 256, F32); c += 1024
    DMZ = sb(c, 256, F32); c += 1024
    G1 = sb(c, 8, F32); c += 32
    G2 = sb(c, 8, F32); c += 32
    PB = sb(c, 4, F32); c += 16
    PSC = sb(c, 4, F32); c += 16
    SINK = sb(c, 8, F32); c += 32
    INVC = sb(c, 64, F32).rearrange("p (g k) -> p g k", g=4); c += 256
    PW = sb(c, 4 * 128, BF16).rearrange("p (g k) -> p g k", g=4); c += 1024
    STAT = sb(c, 512, F32); c += 2048
    assert c <= 201216, c

    XS = [sb(MIX0 + 8192 + i * 4096, 1024, F32) for i in range(3)]
    RL = [sb(201216 + i * 2048, 512, F32) for i in range(2)]
    XB = [sb(205312 + i * 2048, 1024, BF16) for i in range(2)]
    SQJ = sb(209408, 1024, BF16)
    UP = [sb(H0 + i * 8704, 2176, F32) for i in range(2)]
    SS = [sb(H0 + 17408 + i * 8704, 2176, F32) for i in range(2)]
    DT = [sb(H0 + 34816 + i * 4096, 2048, BF16) for i in range(2)]
    WS = [sb(H0 + 49152 + i * 8192, 8 * 512, BF16).rearrange("p (c n) -> p c n", c=8) for i in range(2)]
    ASB = [sb(H0 + i * 16384, 2048, F32).rearrange("p (h k) -> p h k", h=8) for i in range(2)]
    APN = [sb(H0 + i * 16384 + 8192, 2048, BF16).rearrange("p (h k) -> p h k", h=8) for i in range(2)]
    APT = [sb(H0 + i * 16384 + 12288, 2048, BF16).rearrange("p (h k) -> p h k", h=8) for i in range(2)]
    TG = [[sb(H0 + s * 8192 + j * 2048, 512, F32) for j in range(4)] for s in range(2)]
    WG = []
    for s in range(2):
        base = H0 + 32768 + s * 12288
        WG.append(dict(
            gp=sb(base, 8 * 256, BF16).rearrange("p (c n) -> p c n", c=8),
            ga=sb(base + 4096, 8 * 256, BF16).rearrange("p (c n) -> p c n", c=8),
            pp=sb(base + 8192, 4 * 256, BF16).rearrange("p (c n) -> p c n", c=4),
            pa=sb(base + 10240, 4 * 256, BF16).rearrange("p (c n) -> p c n", c=4),
        ))
    STG = []
    for base in (PMT0, MIX0):
        STG.append(dict(
            up=sb(base, 8 * 1024, BF16).rearrange("p (c n) -> p c n", c=8),
            dn=sb(base + 16384, 8 * 1024, BF16).rearrange("p (c n) -> p c n", c=8),
        ))
    A2T = [sb(KT0 + i * 8192, 8 * 512, BF16).rearrange("p (c n) -> p c n", c=8) for i in range(2)]
    GF = sb(WOUT0 + 4096, 1024, F32)
    OUTT = [sb(WOUT0 + 8192 + i * 4096, 1024, F32) for i in range(2)]

    st_ss = [STAT[:, i:i + 1] for i in range(4)]
    st_rs = [STAT[:, 4 + i:5 + i] for i in range(4)]
    NHALF = STAT[:, 8:9]
    st_fix = STAT[:, 16:48]

    def _stset(j):
        return [STAT[:, 64 + 64 * i + 8 * j:72 + 64 * i + 8 * j] for i in range(4)]
    st_mx, st_m8, st_ng, st_rsum, st_es, st_den, st_rd = (_stset(j) for j in range(7))
    ON = [sb(H0 + 57344 + i * 1024, 512, BF16) for i in range(2)]
    MT = sb(H0 + 59392, 1024, F32).rearrange("p (h k) -> p h k", h=8)
    MT2 = sb(H0 + 63488, 512, F32).rearrange("p (h k) -> p h k", h=8)
    P.op("dve", lambda e: e.memset(NHALF, -0.5), w=[NHALF])
    PBS = STAT[:, 48:52]
    NEG1 = STAT[:, 56:64]
    P.op("dve", lambda e: e.memset(NEG1, -1.0), w=[NEG1])

    def isap(v):
        return not isinstance(v, (int, float)) and v is not None

    def mm(out, lhsT, rhs, start=True, stop=True):
        P.op("pe", lambda e: e.matmul(out, lhsT, rhs, start=start, stop=stop),
             r=[lhsT, rhs], w=[out])

    def tp(out, in_):
        P.op("pe", lambda e: e.transpose(out, in_, IDENT), r=[in_, IDENT], w=[out])

    def act(out, in_, func, bias=0.0, scale=1.0, accum=None):
        def fn(e):
            kw = {}
            if accum is not None:
                kw["accum_out"] = accum
            return e.activation(out, in_, func, bias=bias, scale=scale, **kw)
        P.op("act", fn, r=[in_] + [v for v in (bias, scale) if isap(v)], w=[out, accum])

    def tt(eng, out, in0, in1, op):
        P.op(eng, lambda e: e.tensor_tensor(out, in0, in1, op), r=[in0, in1], w=[out])

    def ts(eng, out, in0, s1, s2, op0, op1=None):
        def fn(e):
            if op1 is None:
                return e.tensor_scalar(out, in0, s1, None, op0)
            return e.tensor_scalar(out, in0, s1, s2, op0, op1)
        P.op(eng, fn, r=[in0] + [v for v in (s1, s2) if isap(v)], w=[out])

    def stt(eng, out, in0, scalar, in1, op0, op1):
        P.op(eng, lambda e: e.scalar_tensor_tensor(out, in0, scalar, in1, op0, op1),
             r=[in0, in1] + ([scalar] if isap(scalar) else []), w=[out])

    def cp(eng, out, in_):
        if eng == "act":
            P.op("act", lambda e: e.copy(out, in_), r=[in_], w=[out])
        else:
            P.op(eng, lambda e: e.tensor_copy(out, in_), r=[in_], w=[out])

    def dma(q, out, in_, slot):
        P.op(q, lambda e: e.dma_start(out=out, in_=in_), r=[in_], w=[out], dma_slot=slot)

    dma("pool", IDENT, ident_d, "ident")
    def load_small_consts():
        dma("sp", G1, g1_d, "g1")
        dma("sp", PB, pb_d, "pb")
        dma("sp", PSC, psc_d, "psc")
        dma("sp", INVC, invc_d.rearrange("p (g k) -> p g k", g=4), "invc")
        dma("sp", DM, dm_d, "dm")
        dma("sp", DMZ, dm0_d, "dm0")
        dma("sp", SINK, sink_d, "sink")
        dma("sp", G2, g2_d, "g2")
        tt("dve", PBS, PB, PSC, ALU.mult)

    w_in_v = w_in_d.rearrange("(c p) n -> p c n", p=128)
    for g in range(4):
        dma("pool", WS[0][:, :, g * 128:(g + 1) * 128], w_in_v[:, :, g * 128:(g + 1) * 128], "ws0_%d" % g)
    dma("pool", WS[1][:, :, 0:256], w_in_v[:, :, 1024:1280], "ws1")
    dma("pool", PW, pool_w_d.rearrange("g c d -> c g d"), "pw")

    WS2 = sb(MIX0, 8 * 512, BF16).rearrange("p (c n) -> p c n", c=8)
    dma("pool", WS2, w_in_v[:, :, 512:1024], "ws2")

    def norm_stats(src, k):
        ss, rs = st_ss[k % 4], st_rs[k % 4]
        act(SQJ, src, AF.Square, scale=1.0 / 32.0, accum=ss)
        ts("dve", ss, ss, EPS, None, ALU.add)
        tt("pool", rs, ss, NHALF, ALU.pow)
        return rs

    def norm_transpose(xb, tile_idx, gain, pbank):
        pt = bank_bf(pbank).rearrange("p (c k) -> p c k", c=8)
        for cc in range(8):
            tp(pt[:, cc, :], xb[:, cc * 128:(cc + 1) * 128])
        gb = gain.unsqueeze(2).to_broadcast([128, 8, 128])
        tt("dve", UT[:, tile_idx, :, :], pt, gb, ALU.mult)

    nb = [0]

    def next_bank():
        b = 2 + nb[0] % 6
        nb[0] += 1
        return b

    PIECES = [(0, 128)] + [(128 + 512 * s, 512) for s in range(4)]

    def ut_cols(c0, n, dchunk):
        t0 = c0 // 128
        return UT[:, t0:t0 + n // 128, dchunk, :]

    def evac_copy(dst, src):
        cp("act", dst, src)

    def evac_q(dst, src):
        act(dst, src, AF.Copy, scale=0.125)

    def piece_job(wbuf, wc0, dst_fn, evac_fn, piece, after=None):
        c0, n = piece

        def fn():
            b = next_bank()
            o = bank(b)[:, 0:n]
            for dchunk in range(8):
                mm(o, wbuf[:, dchunk, wc0:wc0 + 128], ut_cols(c0, n, dchunk),
                   start=(dchunk == 0), stop=(dchunk == 7))
            evac_fn(dst_fn(c0, n), o)
            if after is not None:
                after()
        return ((c0 + n) // 128 - 1, fn)

    def pool_ops(g):
        up, d_t = UP[g % 2], DT[g % 2]
        w = POOLW[g]
        eng = "pool" if g < 3 else "dve"
        cur = up
        sh = 1
        k = 0
        while sh < w:
            nxt = SS[k % 2]
            lo = 2 * sh - 1
            tt(eng, nxt[:, lo:2176], cur[:, lo:2176], cur[:, lo - sh:2176 - sh], ALU.add)
            cur = nxt
            sh *= 2
            k += 1
        stt("dve", d_t[:, 0:2048], cur[:, 128:2176], 1.0 / w, up[:, 128:2176], ALU.mult, ALU.subtract)
        fx = st_fix[:, 0:16]
        tt("dve", fx, cur[:, 128:144], INVC[:, g, :], ALU.mult)
        tt("dve", d_t[:, 0:16], fx, up[:, 128:144], ALU.subtract)

    def pool_linear(g):
        d_t = DT[g % 2]
        for s in range(4):
            b = next_bank()
            o = bank(b)
            mm(o, PW[:, g, :], d_t[:, s * 512:(s + 1) * 512])
            act(PMT[:, g, s * 512:(s + 1) * 512], o, AF.Identity, bias=PBS[:, g:g + 1],
                scale=PSC[:, g:g + 1])

    VT = sb(H0 + 43008, 2176, BF16)

    def v_transposes():
        for t0 in range(0, 17, 4):
            nt = min(4, 17 - t0)
            b = next_bank()
            pt = bank_bf(b)[:, 0:nt * 128].rearrange("p (t k) -> p t k", t=nt)
            for j in range(nt):
                tp(pt[:, j, :], VT[:, (t0 + j) * 128:(t0 + j + 1) * 128])
            evac_copy(V[:, t0:t0 + nt, :], pt)

    def v_chunk(p):
        return piece_job(WS[1], 128, lambda c0, n: VT[:, c0:c0 + n], evac_copy, PIECES[p],
                         after=v_transposes if p == 4 else None)

    def v_job(t0):
        nt = min(4, 17 - t0)

        def fn():
            b = next_bank()
            o = bank(b)
            for j in range(nt):
                for dchunk in range(8):
                    mm(o[:, j * 128:(j + 1) * 128], UT[:, t0 + j, dchunk, :], WS[1][:, dchunk, 128:256],
                       start=(dchunk == 0), stop=(dchunk == 7))
            evac_copy(V[:, t0:t0 + nt, :], o[:, 0:nt * 128].rearrange("p (t k) -> p t k", t=nt))
        return (t0 + nt - 1, fn)

    def ks_swap():
        dma("sp", KS[64:128, :], KT[0:64, :], "ksa")
        dma("sp", KS[0:64, :], KT[64:128, :], "ksb")

    def pool_chunk(g, p):
        return piece_job(WS[0], g * 128, lambda c0, n, g=g: UP[g % 2][:, c0:c0 + n], evac_copy,
                         PIECES[p], after=(lambda g=g: pool_ops(g)) if p == 4 else None)

    def k_chunk(p):
        return piece_job(WS[1], 0, lambda c0, n: KT[:, c0:c0 + n], evac_copy, PIECES[p],
                         after=ks_swap if p == 4 else None)

    def q_chunk(qc, p):
        return piece_job(WS2, qc * 128, lambda c0, n, qc=qc: QT[:, qc, c0 - 128:c0 - 128 + n],
                         evac_q, PIECES[p])

    jobs = []
    for p in range(5):
        jobs.append(pool_chunk(0, p))
        jobs.append(k_chunk(p))
        jobs.append(v_chunk(p))
        jobs.append(pool_chunk(1, p))
        if p >= 1:
            jobs.append(q_chunk(0, p))
            jobs.append(q_chunk(1, p))
    for p in range(4):
        jobs.append(pool_chunk(2, p))
    jobs.append((16, lambda: pool_linear(0)))
    jobs.append(pool_chunk(2, 4))
    for p in range(4):
        jobs.append(pool_chunk(3, p))
    jobs.append((16, lambda: pool_linear(1)))
    jobs.append(pool_chunk(3, 4))
    for p in range(1, 5):
        jobs.append(q_chunk(2, p))
    jobs.append((16, lambda: pool_linear(2)))
    for p in range(1, 5):
        jobs.append(q_chunk(3, p))
    jobs.append((16, lambda: pool_linear(3)))

    ji = 0
    for t in range(19):
        if t < 17:
            xs = XS[t % 3]
            src = xh_d if t == 0 else x_d[(t - 1) * 128:t * 128, :]
            dma("sp", xs, src, "xs%d" % (t % 3))
            act(SQJ, xs, AF.Square, scale=1.0 / 32.0, accum=st_ss[t % 4])
        if t == 2:
            load_small_consts()
        if t >= 2:
            u = t - 2
            norm_transpose(XB[u % 2], u, G1, u % 2)
        if 1 <= t <= 17:
            u = t - 1
            ts("dve", XB[u % 2], XS[u % 3], st_rs[u % 4], None, ALU.mult)
        if t < 17:
            ts("dve", st_ss[t % 4], st_ss[t % 4], EPS, None, ALU.add)
            tt("pool", st_rs[t % 4], st_ss[t % 4], NHALF, ALU.pow)
        if t >= 2:
            if ji < len(jobs) and jobs[ji][0] <= t - 2:
                jobs[ji][1]()
                ji += 1
    phase_end('A')
    while ji < len(jobs):
        jobs[ji][1]()
        ji += 1

    phase_end('C')
    p_pool_v = p_pool_d.rearrange("(c p) n -> p c n", p=128)
    p_attn_v = p_attn_d.rearrange("(c p) n -> p c n", p=128)

    def load_pair(pr):
        wg = WG[pr % 2]
        s = "wg%d" % (pr % 2)
        dma("pool", wg["gp"], w_in_v[:, :, 1280 + pr * 256:1280 + (pr + 1) * 256], s + "gp")
        dma("pool", wg["ga"], w_in_v[:, :, 2304 + pr * 256:2304 + (pr + 1) * 256], s + "ga")
        dma("pool", wg["pp"], p_pool_v[:, :, pr * 256:(pr + 1) * 256], s + "pp")
        dma("pool", wg["pa"], p_attn_v[:, :, pr * 256:(pr + 1) * 256], s + "pa")

    load_pair(0)
    load_pair(1)
    dma("pool", WOUT, w_out_d.rearrange("(c p) n -> p c n", p=128), "wout")

    BIAS = sb(MIX0, 2048, F32).rearrange("p (h k) -> p h k", h=8)
    BIASZ = sb(MIX0 + 20480, 2048, F32).rearrange("p (h k) -> p h k", h=8)
    if FUSED_MAX or BIAS_TT:
        for h in range(8):
            ts("dve", BIAS[:, h, :], DM, -SLOPES[h], None, ALU.mult)
            ts("dve", BIASZ[:, h, :], DMZ, -SLOPES[h], None, ALU.mult)

    def att_s1(b):
        asb = ASB[b % 2]
        dmt = DMZ if b == 0 else DM
        for h in range(8):
            cc, r, g = h // 2, h % 2, h // 4
            ksrc = KT if r == g else KS
            o = bank(h % 4)[:, (h // 4) * 256:(h // 4) * 256 + 256]
            mm(o, QT[r * 64:(r + 1) * 64, cc, b * 128:(b + 1) * 128],
               ksrc[r * 64:(r + 1) * 64, b * 128:b * 128 + 256])
        mx = st_mx[b % 4]
        if FUSED_MAX:
            bt = BIASZ if b == 0 else BIAS
            for h in range(8):
                o = bank(h % 4)[:, (h // 4) * 256:(h // 4) * 256 + 256]
                P.op("dve", lambda e, h=h, o=o: e.tensor_tensor_reduce(
                    out=asb[:, h, :], in0=o, in1=bt[:, h, :], scale=1.0, scalar=-3.0e38,
                    op0=ALU.add, op1=ALU.max, accum_out=mx[:, h:h + 1]),
                    r=[o, bt[:, h, :]], w=[asb[:, h, :], mx[:, h:h + 1]])
        elif BIAS_TT:
            bt = BIASZ if b == 0 else BIAS
            for j in range(4):
                tt("dve", asb[:, j:8:4, :], bank(j).rearrange("p (h k) -> p h k", h=2), bt[:, j:8:4, :],
                   ALU.add)
            P.op("dve", lambda e: e.reduce_max(mx, asb, AX.X), r=[asb], w=[mx])
        else:
            for h in range(8):
                o = bank(h % 4)[:, (h // 4) * 256:(h // 4) * 256 + 256]
                stt("dve", asb[:, h, :], dmt, -SLOPES[h], o, ALU.mult, ALU.add)
            if TS_MAX:
                for h in range(8):
                    P.op("dve", lambda e, h=h: e.tensor_scalar(asb[:, h, :], asb[:, h, :], 1.0, None,
                                                              ALU.mult, ALU.max, accum_out=mx[:, h:h + 1]),
                         r=[asb[:, h, :]], w=[asb[:, h, :], mx[:, h:h + 1]])
            else:
                P.op("dve", lambda e: e.reduce_max(mx, asb, AX.X), r=[asb], w=[mx])

    def att_tiny(b):
        k4 = b % 4
        mx, m8, ng, es = st_mx[k4], st_m8[k4], st_ng[k4], st_es[k4]
        tt("dve", m8, mx, SINK, ALU.max)
        tt("pool", ng, m8, NEG1, ALU.mult)
        tt("pool", es, SINK, m8, ALU.subtract)

    def att_exp(b):
        k4 = b % 4
        asb, apb = ASB[b % 2], APN[b % 2]
        ng, rsum, es = st_ng[k4], st_rsum[k4], st_es[k4]
        for h in range(8):
            act(apb[:, h, :], asb[:, h, :], AF.Exp, bias=ng[:, h:h + 1], accum=rsum[:, h:h + 1])
        act(es, es, AF.Exp)

    def att_tr(b):
        apb = APN[b % 2]
        for half in range(2):
            pt = bank_bf(4 + half).rearrange("p (h k) -> p h k", h=4)
            for hh in range(4):
                h = half * 4 + hh
                for kc in range(2):
                    tp(pt[:, hh, kc * 128:(kc + 1) * 128], apb[:, h, kc * 128:(kc + 1) * 128])

    def att_ptevac(b):
        apt = APT[b % 2]
        for half in range(2):
            pt = bank_bf(4 + half).rearrange("p (h k) -> p h k", h=4)
            cp("dve" if (FUSED_MAX and half == 1) else "act", apt[:, half * 4:half * 4 + 4, :], pt)

    def att_pv(b):
        k4 = b % 4
        apt = APT[b % 2]
        rsum, es, den, rd = st_rsum[k4], st_es[k4], st_den[k4], st_rd[k4]
        ob = bank(6)
        for h in range(8):
            g = h // 4
            for kc in range(2):
                mm(ob[:, h * 64:(h + 1) * 64], apt[:, h, kc * 128:(kc + 1) * 128],
                   V[:, b + kc, g * 64:(g + 1) * 64], start=(kc == 0), stop=(kc == 1))
        tt("pool", den, rsum, es, ALU.add)
        P.op("dve", lambda e: e.reciprocal(rd, den), r=[den], w=[rd])

    def att_s4_on(b):
        on, rd = ON[b % 2], st_rd[b % 4]
        ob = bank(6)
        tt("dve", on.rearrange("p (h d) -> p h d", h=8), ob.rearrange("p (h d) -> p h d", h=8),
           rd.unsqueeze(2).to_broadcast([128, 8, 64]), ALU.mult)

    def att_s4_t(b):
        on = ON[b % 2]
        pt7 = bank_bf(7)[:, 0:512].rearrange("p (c k) -> p c k", c=4)
        for cc in range(4):
            tp(pt7[:, cc, :], on[:, cc * 128:(cc + 1) * 128])

    def att_s4_evac(b):
        pt7 = bank_bf(7)[:, 0:512].rearrange("p (c k) -> p c k", c=4)
        cp("dve", QT[:, :, b * 128:(b + 1) * 128], pt7)

    def blk(j):
        return 0 <= j < 16

    for i in range(20):
        if blk(i - 1):
            att_tiny(i - 1)
        if blk(i - 2):
            att_ptevac(i - 2)
        if blk(i - 1):
            att_exp(i - 1)
        if blk(i - 3):
            att_s4_on(i - 3)
        if blk(i):
            att_s1(i)
        if blk(i - 3):
            att_s4_t(i - 3)
        if blk(i - 2):
            att_pv(i - 2)
        if blk(i - 1):
            att_tr(i - 1)
        if blk(i - 3):
            att_s4_evac(i - 3)
    OT = QT
    phase_end('D')

    for pr in range(4):
        wg = WG[pr % 2]
        for mmi in range(2):
            m = pr * 2 + mmi
            wc = mmi * 128
            for s in range(4):
                st = (m * 4 + s) % 2
                bgp, bga, byp, bya = (bank(st * 4 + i) for i in range(4))
                for dchunk in range(8):
                    mm(bgp, wg["gp"][:, dchunk, wc:wc + 128], UT[:, 1 + 4 * s:5 + 4 * s, dchunk, :],
                       start=(dchunk == 0), stop=(dchunk == 7))
                for dchunk in range(8):
                    mm(bga, wg["ga"][:, dchunk, wc:wc + 128], UT[:, 1 + 4 * s:5 + 4 * s, dchunk, :],
                       start=(dchunk == 0), stop=(dchunk == 7))
                for j in range(4):
                    mm(byp, wg["pp"][:, j, wc:wc + 128], PMT[:, j, s * 512:(s + 1) * 512],
                       start=(j == 0), stop=(j == 3))
                for j in range(4):
                    mm(bya, wg["pa"][:, j, wc:wc + 128], OT[:, j, s * 512:(s + 1) * 512],
                       start=(j == 0), stop=(j == 3))
                tg = TG[st]
                act(tg[0], bgp, AF.Tanh, scale=0.5)
                act(tg[1], bga, AF.Tanh, scale=0.5)
                stt("dve", tg[2], tg[0], 1.0, byp, ALU.add, ALU.mult)
                stt("dve", tg[3], tg[1], 1.0, bya, ALU.add, ALU.mult)
                tt("pool", MIXT[:, 4 * s:4 * s + 4, m, :],
                   tg[2].rearrange("p (t k) -> p t k", t=4),
                   tg[3].rearrange("p (t k) -> p t k", t=4), ALU.add)
        if pr + 2 < 4:
            load_pair(pr + 2)

    phase_end('E')
    w_up_v = w_up_d.rearrange("(c p) n -> p c n", p=128)
    w_dn_v = w_dn_d.rearrange("(c p) n -> p c n", p=128)

    def load_stage(sg):
        buf = STG[sg % 2]
        dma("pool", buf["up"], w_up_v[:, :, sg * 1024:(sg + 1) * 1024], "stg%dup" % (sg % 2))
        dma("pool", buf["dn"], w_dn_v[:, sg * 8:(sg + 1) * 8, :], "stg%ddn" % (sg % 2))

    load_stage(0)

    y_v = y_d.rearrange("(t p) d -> t p d", p=128)
    out_slots = []

    def mlp_up(sg, s):
        buf = STG[sg % 2]
        a2 = A2T[(sg * 4 + s) % 2]
        for fc in range(8):
            o = bank(fc % 4)
            for dchunk in range(8):
                mm(o, buf["up"][:, dchunk, fc * 128:(fc + 1) * 128],
                   UT[:, 1 + 4 * s:5 + 4 * s, dchunk, :], start=(dchunk == 0), stop=(dchunk == 7))
            rl = RL[fc % 2]
            act(rl, o, AF.Relu)
            tt("pool" if fc % 2 else "dve", a2[:, fc, :], rl, rl, ALU.mult)

    pending = []

    def emit_final(t, rs):
        ot = OUTT[t % 2]
        stt("dve", ot, Hbuf[:, t, :], rs, GF, ALU.mult, ALU.mult)
        slot = "out%d" % (t % 2)
        dma("sp", y_v[t], ot, slot)
        if slot not in out_slots:
            out_slots.append(slot)

    def mlp_down(sg, s):
        buf = STG[sg % 2]
        a2 = A2T[(sg * 4 + s) % 2]
        k = 0
        for ti in range(4):
            t = 4 * s + ti
            for hh in range(2):
                o = bank(4 + k % 4)
                k += 1
                for fc in range(8):
                    mm(o, a2[:, fc, ti * 128:(ti + 1) * 128], buf["dn"][:, fc, hh * 512:(hh + 1) * 512],
                       start=(fc == 0), stop=(fc == 7))
                hv = Hbuf[:, t, hh * 512:(hh + 1) * 512]
                tt("dve", hv, o, hv, ALU.add)
            if sg == 3:
                rs = norm_stats(Hbuf[:, t, :], t)
                if pending:
                    emit_final(*pending.pop())
                pending.append((t, rs))

    rs_f = {}
    for t in range(17):
        if t < 16:
            dma("sp", Hbuf[:, t, :], x_d[t * 128:(t + 1) * 128, :], "h%d" % t)
            for hh in range(2):
                o = bank(2 * (t % 3) + hh)
                for j in range(8):
                    mm(o, MIXT[:, t, j, :], WOUT[:, j, hh * 512:(hh + 1) * 512],
                       start=(j == 0), stop=(j == 7))
                hv = Hbuf[:, t, hh * 512:(hh + 1) * 512]
                stt("dve", hv, o, 0.5, hv, ALU.mult, ALU.add)
        if t == 16:
            mlp_up(0, 0)
        if t >= 1:
            u = t - 1
            act(XB[u % 2], Hbuf[:, u, :], AF.Copy, scale=rs_f[u])
            norm_transpose(XB[u % 2], u + 1, G2, u % 2 + 6)
        if t < 16:
            rs_f[t] = norm_stats(Hbuf[:, t, :], t)
    phase_end('F')
    load_stage(1)
    dma("sp", GF, gf_d, "gf")

    seq = [(sg, s) for sg in range(4) for s in range(4)]
    for i in range(len(seq) + 1):
        if 1 <= i < len(seq):
            mlp_up(*seq[i])
        if i >= 1:
            sg, s = seq[i - 1]
            mlp_down(sg, s)
            if s == 3 and sg + 2 < 4:
                load_stage(sg + 2)

    while pending:
        emit_final(*pending.pop())
    finish(out_slots)
    return nc


_CACHE = {}


def _consts():
    qi = np.arange(128)[:, None]
    kj = np.arange(256)[None, :]
    dist = (128 + qi - kj).astype(np.float32)
    valid = (dist >= 0) & (dist < 128)
    dm = np.where(valid, dist, np.float32(1e32)).astype(np.float32)
    dm0 = dm.copy()
    dm0[:, :128] = np.float32(1e32)
    ident = np.eye(128, dtype=np.float32)
    inv_mid = np.zeros((128, 4, 16), np.float32)
    inv_first = np.zeros((128, 4, 16), np.float32)
    for g, w in enumerate(POOLW):
        inv_mid[:, g, :] = np.float32(1.0) / np.float32(w)
        cnt = np.minimum(np.arange(16) + 1, w).astype(np.float32)
        inv_first[:, g, :] = (np.float32(1.0) / cnt)[None, :]
    return dm, dm0, ident, inv_mid.reshape(128, 64), inv_first.reshape(128, 64)


def kernel(x, norm_mix, w_in, pool_w, pool_b, pool_scale, attn_sinks, p_pool, p_attn,
           w_out, norm_mlp, w_up, w_down, norm_final):
    if "nc" not in _CACHE:
        _CACHE["nc"] = build_program()
    nc = _CACHE["nc"]
    in_maps = make_in_maps(x, norm_mix, w_in, pool_w, pool_b, pool_scale, attn_sinks, p_pool, p_attn,
                           w_out, norm_mlp, w_up, w_down, norm_final)
    res = run_bass_kernel_spmd(nc, in_maps, core_ids=list(range(NCORES)))
    out = np.empty((2, 4 * T, D), np.float32)
    for i in range(NCORES):
        b, ch = i // 4, i % 4
        out[b, ch * T:(ch + 1) * T] = np.asarray(res.results[i]["y"], dtype=np.float32).reshape(T, D)
    return out


def make_in_maps(x, norm_mix, w_in, pool_w, pool_b, pool_scale, attn_sinks, p_pool, p_attn,
                 w_out, norm_mlp, w_up, w_down, norm_final):
    f = lambda a: np.ascontiguousarray(np.asarray(a, dtype=np.float32))
    x = f(x)
    dm, dm0, ident, inv_mid, inv_first = _consts()
    shared = {
        "w_in": f(w_in[0]), "pool_w": f(pool_w[0]), "p_pool": f(p_pool[0]), "p_attn": f(p_attn[0]),
        "w_out": f(w_out[0]), "w_up": f(w_up[0]), "w_down": f(w_down[0]),
        "g1t": f(np.asarray(norm_mix[0]).reshape(8, 128).T),
        "g2t": f(np.asarray(norm_mlp[0]).reshape(8, 128).T),
        "gfb": f(np.broadcast_to(np.asarray(norm_final)[None, :], (128, D))),
        "pbt": f(np.asarray(pool_b[0]).reshape(4, 128).T),
        "psct": f(np.asarray(pool_scale[0]).reshape(4, 128).T),
        "sinkb": f(np.broadcast_to(np.asarray(attn_sinks[0])[None, :], (128, 8))),
        "ident": ident, "dm": dm,
    }
    in_maps = []
    for i in range(NCORES):
        b, ch = i // 4, i % 4
        m = dict(shared)
        m["x"] = f(x[b, ch * T:(ch + 1) * T])
        if ch == 0:
            m["xh"] = np.zeros((128, D), np.float32)
            m["dm0"] = dm0
            m["invc"] = inv_first
        else:
            m["xh"] = f(x[b, ch * T - 128:ch * T])
            m["dm0"] = dm
            m["invc"] = inv_mid
        in_maps.append(m)
    return in_maps
```
